# Optimizing a Trainium2 kernel written in Bass

```python
import jax, jax.numpy as jnp
from jax import lax
import numpy as np

D_MODEL = 4096
BATCH = 1
SEQ = 8192
DEPTH = 1
DEC_BATCH = 32
DEC_SEQ = 32
PAST_LEN = 2048

CHUNK = 64
W_CONV = D_MODEL // 2
CONV_K = 31
W_POOL = D_MODEL // 2
POOL_WINDOWS = (2, 4, 8, 16)
N_POOL_GROUPS = len(POOL_WINDOWS)
POOL_GC = W_POOL // N_POOL_GROUPS
POOL_GO = D_MODEL // N_POOL_GROUPS
POOL_MAX = max(POOL_WINDOWS)
D_FF = ((8 * D_MODEL // 3 + 255) // 256) * 256
D_IN = 2 * W_CONV + W_POOL + 2 * D_MODEL
EPS = 1e-6

kernel_name = "gated_conv_pool_streaming_encoder"


def _rmsnorm(x, g):
    xf = x.astype(jnp.float32)
    y = xf * lax.rsqrt(jnp.mean(xf * xf, axis=-1, keepdims=True) + EPS)
    return (y * g.astype(jnp.float32)).astype(x.dtype)


def _layernorm(x, g, b):
    xf = x.astype(jnp.float32)
    mu = jnp.mean(xf, axis=-1, keepdims=True)
    var = jnp.mean(jnp.square(xf - mu), axis=-1, keepdims=True)
    y = (xf - mu) * lax.rsqrt(var + EPS)
    return (y * g.astype(jnp.float32) + b.astype(jnp.float32)).astype(x.dtype)


def _depthwise_causal_conv(full, w, b):
    c = full.shape[-1]
    out = lax.conv_general_dilated(
        full, w[:, None, :].astype(full.dtype), window_strides=(1,), padding="VALID",
        dimension_numbers=("NWC", "WIO", "NWC"), feature_group_count=c)
    return out + b.astype(full.dtype)


def _multiscale_pool(full, t_new, pos0):
    dt = full.dtype
    ff = full.astype(jnp.float32)
    cs = jnp.cumsum(ff, axis=1)
    cs = jnp.concatenate([jnp.zeros_like(cs[:, :1]), cs], axis=1)
    pos = (pos0 + jnp.arange(t_new)).astype(jnp.float32)[None, :, None]
    x_tok = ff[:, POOL_MAX - 1:]
    outs = []
    for g, w in enumerate(POOL_WINDOWS):
        sl = slice(g * POOL_GC, (g + 1) * POOL_GC)
        s = cs[:, POOL_MAX:POOL_MAX + t_new, sl] - cs[:, POOL_MAX - w:POOL_MAX - w + t_new, sl]
        cnt = jnp.minimum(jnp.float32(w), pos + 1.0)
        outs.append(s / cnt - x_tok[..., sl])
    return jnp.concatenate(outs, axis=-1).astype(dt)


def _layer(h, c, conv_hist, pool_hist, pos0, w_ada, b_ada, g_norm1, w_in, conv_w, conv_b,
           ln_g, ln_b, w_conv_out, w_pool, pool_scale, w_out, g_norm2, w_ffn_in, w_ffn_out):
    bsz, t_new, _ = h.shape
    mod = (jax.nn.silu(c) @ w_ada + b_ada)[:, None, :]
    shift1, scale1, gate1, shift2, scale2, gate2 = jnp.split(mod, 6, axis=-1)

    u = _rmsnorm(h, g_norm1) * (1 + scale1) + shift1
    z = u @ w_in
    za, zg, zp, ga, gb = jnp.split(
        z, [W_CONV, 2 * W_CONV, 2 * W_CONV + W_POOL, 2 * W_CONV + W_POOL + D_MODEL], axis=-1)

    a_in = za * jax.nn.sigmoid(zg)
    conv_full = jnp.concatenate([conv_hist.astype(a_in.dtype), a_in], axis=1)
    a = _depthwise_causal_conv(conv_full, conv_w, conv_b)
    a = jax.nn.silu(_layernorm(a, ln_g, ln_b)) @ w_conv_out

    pool_full = jnp.concatenate([pool_hist.astype(zp.dtype), zp], axis=1)
    p = _multiscale_pool(pool_full, t_new, pos0)
    p = jnp.einsum("btgc,gcd->btgd", p.reshape(bsz, t_new, N_POOL_GROUPS, POOL_GC), w_pool)
    p = p.reshape(bsz, t_new, D_MODEL) * pool_scale

    m = jax.nn.sigmoid(ga) * a + jax.nn.sigmoid(gb) * p
    h = h + gate1 * (m @ w_out)

    u2 = _rmsnorm(h, g_norm2) * (1 + scale2) + shift2
    gu = u2 @ w_ffn_in
    f_g, f_u = jnp.split(gu, 2, axis=-1)
    h = h + gate2 * ((jax.nn.silu(f_g) * f_u) @ w_ffn_out)

    new_conv = conv_full[:, -(CONV_K - 1):]
    new_pool = pool_full[:, -(POOL_MAX - 1):]
    return h, new_conv, new_pool


def setup_inputs(seed: int = 0) -> dict:
    key = jax.random.key(seed)
    ks = jax.random.split(key, 24)
    nrm = lambda k, shape, s: jax.random.normal(k, shape, jnp.float32) * s
    L = DEPTH
    return {
        "x_prompt": nrm(ks[0], (BATCH, SEQ, D_MODEL), 1.0),
        "x_sample": nrm(ks[1], (DEC_BATCH, DEC_SEQ, D_MODEL), 1.0),
        "state_conv": nrm(ks[2], (L, DEC_BATCH, CONV_K - 1, W_CONV), 0.5),
        "state_pool": nrm(ks[3], (L, DEC_BATCH, POOL_MAX - 1, W_POOL), 1.0),
        "c_prompt": nrm(ks[4], (BATCH, D_MODEL), 1.0),
        "c_sample": nrm(ks[5], (DEC_BATCH, D_MODEL), 1.0),
        "w_ada": nrm(ks[6], (L, D_MODEL, 6 * D_MODEL), 0.2 * D_MODEL ** -0.5),
        "b_ada": nrm(ks[7], (L, 6 * D_MODEL), 0.02),
        "g_norm1": 1.0 + nrm(ks[8], (L, D_MODEL), 0.02),
        "w_in": nrm(ks[9], (L, D_MODEL, D_IN), D_MODEL ** -0.5),
        "conv_w": nrm(ks[10], (L, CONV_K, W_CONV), CONV_K ** -0.5),
        "conv_b": nrm(ks[11], (L, W_CONV), 0.02),
        "ln_g": 1.0 + nrm(ks[12], (L, W_CONV), 0.02),
        "ln_b": nrm(ks[13], (L, W_CONV), 0.02),
        "w_conv_out": nrm(ks[14], (L, W_CONV, D_MODEL), W_CONV ** -0.5),
        "w_pool": nrm(ks[15], (L, N_POOL_GROUPS, POOL_GC, POOL_GO), POOL_GC ** -0.5),
        "pool_scale": 1.0 + nrm(ks[16], (L, D_MODEL), 0.02),
        "w_out": nrm(ks[17], (L, D_MODEL, D_MODEL), D_MODEL ** -0.5),
        "g_norm2": 1.0 + nrm(ks[18], (L, D_MODEL), 0.02),
        "w_ffn_in": nrm(ks[19], (L, D_MODEL, 2 * D_FF), D_MODEL ** -0.5),
        "w_ffn_out": nrm(ks[20], (L, D_FF, D_MODEL), D_FF ** -0.5),
        "g_final": 1.0 + nrm(ks[21], (D_MODEL,), 0.02),
    }


def reference(x_prompt, x_sample, state_conv, state_pool, c_prompt, c_sample, w_ada, b_ada,
              g_norm1, w_in, conv_w, conv_b, ln_g, ln_b, w_conv_out, w_pool, pool_scale,
              w_out, g_norm2, w_ffn_in, w_ffn_out, g_final):
    hp, hs = x_prompt, x_sample
    bp = x_prompt.shape[0]
    conv_p, pool_p, conv_s, pool_s = [], [], [], []
    for l in range(DEPTH):
        params = (w_ada[l], b_ada[l], g_norm1[l], w_in[l], conv_w[l], conv_b[l], ln_g[l], ln_b[l],
                  w_conv_out[l], w_pool[l], pool_scale[l], w_out[l], g_norm2[l], w_ffn_in[l],
                  w_ffn_out[l])
        zc = jnp.zeros((bp, CONV_K - 1, W_CONV), hp.dtype)
        zp = jnp.zeros((bp, POOL_MAX - 1, W_POOL), hp.dtype)
        hp, nc, npl = _layer(hp, c_prompt, zc, zp, 0, *params)
        conv_p.append(nc)
        pool_p.append(npl)
        hs, nc, npl = _layer(hs, c_sample, state_conv[l], state_pool[l], PAST_LEN, *params)
        conv_s.append(nc)
        pool_s.append(npl)
    y_prompt = _rmsnorm(hp, g_final)
    y_sample = _rmsnorm(hs, g_final)
    new_state_conv_prompt = jnp.stack(conv_p)
    new_state_pool_prompt = jnp.stack(pool_p)
    new_state_conv_sample = jnp.stack(conv_s)
    new_state_pool_sample = jnp.stack(pool_s)
    return (y_prompt, y_sample, new_state_conv_prompt, new_state_pool_prompt,
            new_state_conv_sample, new_state_pool_sample)
```

```python
import contextlib
import numpy as np
import concourse.bass as bass
import concourse.mybir as mybir
from concourse.bass_utils import run_bass_kernel_spmd

F32 = mybir.dt.float32
BF16 = mybir.dt.bfloat16
AF = mybir.ActivationFunctionType
ALU = mybir.AluOpType

NCORES = 8
CONV_K = 31
POOL_W = (2, 4, 8, 16)
EPS = 1e-6
NS = 3
SLOT = 8192
NPS = 4
ADA_A = 5
ADA_B = 2


class Buf:
    __slots__ = ("name", "w", "r")

    def __init__(self, name):
        self.name = name
        self.w = None
        self.r = {}


class Eng:
    def __init__(self, name):
        self.name = name
        self.q = []
        self.cnt = 0
        self.seen = {}


class Ctx:
    def __init__(self):
        self.eng = {n: Eng(n) for n in ("pe", "act", "dve", "pool", "sp")}
        self.dmacnt = {}

    def emit(self, en, fn, reads=(), writes=(), sig=True, dma=None):
        eng = self.eng[en]
        need = {}

        def add(k, v):
            if need.get(k, 0) < v:
                need[k] = v

        for b in reads:
            if b.w is not None:
                add(*b.w)
        for b in writes:
            if b.w is not None:
                add(*b.w)
            for k, v in b.r.items():
                add(k, v)
        waits = []
        for k, v in need.items():
            if eng.seen.get(k, 0) < v:
                eng.seen[k] = v
                waits.append((k, v))
        if dma is not None:
            self.dmacnt[dma] = self.dmacnt.get(dma, 0) + 16
            ev = (dma, self.dmacnt[dma])
            inc = (dma, 16)
        elif sig:
            eng.cnt += 1
            ev = (en, eng.cnt)
            inc = (en, 1)
        else:
            ev = None
            inc = None
        eng.q.append((waits, fn, inc))
        if ev is not None:
            for b in writes:
                b.w = ev
                b.r = {}
            for b in reads:
                if b.r.get(ev[0], 0) < ev[1]:
                    b.r[ev[0]] = ev[1]
        return ev

    def group(self, en, fns, reads=(), writes=()):
        n = len(fns)
        for i, fn in enumerate(fns):
            last = i == n - 1
            if i == 0 and not last:
                self.emit(en, fn, reads, writes, sig=False)
            elif last:
                self.emit(en, fn, reads, writes, sig=True)
            else:
                self.eng[en].q.append(((), fn, None))

    @staticmethod
    def switch(old, new):
        ev = {}
        for b in old:
            if b.w is not None and ev.get(b.w[0], 0) < b.w[1]:
                ev[b.w[0]] = b.w[1]
            for k, v in b.r.items():
                if ev.get(k, 0) < v:
                    ev[k] = v
        for b in new:
            b.w = None
            b.r = dict(ev)


def build(D, DFF, TP, NT):
    DC = D // 128
    WC = D // 2
    WCC = WC // 128
    GC = WC // 4
    GCC = GC // 128
    GO = D // 4
    GOC = GO // 128
    FC = DFF // 128
    DIN = 3 * WC + 2 * D
    SQW = 64
    T = TP + SQW
    TH = T + 32
    CBW = 32 + TP + 128
    P1 = 96
    assert TP <= 512 and DC * 5 <= 160 and WCC % 2 == 0 and DC % 8 == 0

    MRW = max(DC * T // 2, 6 * CBW + 2 * TH + 2 * T + 4 * (CBW - 32), 8 * TH)
    FB = min(FC, (MRW - 2 * T) // T)
    if FB < FC:
        pass
    MRW = max(MRW, min(FB, FC) * T + 2 * T)
    assert FB * 512 <= SLOT and DC * 256 <= SLOT

    off = {}
    o = 0
    for name, n in (("cT", DC * 5), ("bada", 6 * DC), ("g1", DC), ("g2", DC), ("gf", DC), ("psc", DC),
                    ("cw", WCC * CONV_K), ("cb", WCC), ("lng", WCC), ("lnb", WCC),
                    ("hm", NT), ("invc", NT * 4 * 16), ("eps", 1)):
        off[name] = o
        o += n
    NCST = o

    nc = bass.Bass("TRN2", target_bir_lowering=False)
    dt = nc.dram_tensor
    xT = dt("xT", [NT, 128, DC, TH], F32, kind="ExternalInput").ap()
    cst = dt("cst", [128, NCST], F32, kind="ExternalInput").ap()
    stc = dt("stc", [NT, 128, WCC, 2, 32], F32, kind="ExternalInput").ap()
    stp = dt("stp", [NT, 128, WCC, 2, 32], F32, kind="ExternalInput").ap()
    w_ada = dt("w_ada", [D, 6 * D], F32, kind="ExternalInput").ap()
    w_in = dt("w_in", [D, DIN], F32, kind="ExternalInput").ap()
    w_co = dt("w_co", [WC, D], F32, kind="ExternalInput").ap()
    w_pl = dt("w_pl", [4 * GC, GO], F32, kind="ExternalInput").ap()
    w_out = dt("w_out", [D, D], F32, kind="ExternalInput").ap()
    w_fi = dt("w_fi", [D, 2 * DFF], F32, kind="ExternalInput").ap()
    w_fo = dt("w_fo", [DFF, D], F32, kind="ExternalInput").ap()
    yT = dt("yT", [NT, 128, DC, T], F32, kind="ExternalOutput").ap()
    oc_s = dt("oc_s", [NT, 128, WCC, 2, 30], F32, kind="ExternalOutput").ap()
    op_s = dt("op_s", [NT, 128, WCC, 2, 15], F32, kind="ExternalOutput").ap()
    oc_p = dt("oc_p", [NT, 128, WCC, 30], F32, kind="ExternalOutput").ap()
    op_p = dt("op_p", [NT, 128, WCC, 15], F32, kind="ExternalOutput").ap()

    K = Ctx()
    with contextlib.ExitStack() as es:
        sb = lambda name, shape, dtype: es.enter_context(nc.sbuf_tensor(name, shape, dtype))
        CST = sb("CST", [128, NCST], F32)
        MOD = sb("MOD", [128, 6, 5, DC], F32)
        ONES_D = sb("ONES_D", [128, 128], F32)
        ONES_W = sb("ONES_W", [128, 128], F32)
        SCb = sb("SCb", [128, DC * 5], BF16)
        U = sb("U", [128, DC, TH], BF16)
        HR = sb("HR", [128, DC * T], F32)
        MR = sb("MR", [128, MRW], F32)
        WR = sb("WR", [128, NS, SLOT], BF16)
        PS = es.enter_context(nc.psum_tensor("PS", [128, 8, 512], F32))

        def cs(name, i=0, n=1):
            return CST[:, off[name] + i: off[name] + i + n]

        HRb = HR[:, WCC * T: 2 * WCC * T].bitcast(BF16)
        MRb = MR[:, :].bitcast(BF16)

        def Hc(c, a=0, b=None):
            b = T if b is None else b
            return HR[:, c * T + a: c * T + b]

        def ACTC(j, a=0, b=None):
            b = T if b is None else b
            return HRb[:, j * T + a: j * T + b]

        def Pc(j, a=0, b=None):
            b = T if b is None else b
            return HRb[:, (WCC + j) * T + a: (WCC + j) * T + b]

        def Mc(c, a=0, b=None):
            b = T if b is None else b
            return MRb[:, c * T + a: c * T + b]

        def MRs(o0, n):
            return MR[:, o0: o0 + n]

        CSTb, MODb = Buf("cst"), [Buf(f"mod{i}") for i in range(6)]
        ONESb, SCbb = Buf("ones"), Buf("scb")
        Ub = [Buf(f"U{c}") for c in range(DC)]
        Hb = [Buf(f"H{c}") for c in range(DC)]
        Wb = [Buf(f"W{s}") for s in range(NS)]
        PSb = [Buf(f"PS{s}") for s in range(NPS)]
        ACTCb = lambda j: Hb[WCC + j // 2]
        Pb = lambda j: Hb[WCC + WCC // 2 + j // 2]
        nXB, nTB = 4, min(8, WCC)
        R_norm = dict(XB=[Buf(f"XB{i}") for i in range(nXB)], SQ=[Buf("SQ0"), Buf("SQ1")],
                      ACC=[Buf("SQACC")], RS=[Buf("RSTD")])
        R_A = dict(CB=[Buf("CB0"), Buf("CB1")], PB=[Buf("PB0"), Buf("PB1")], S=[Buf("S0"), Buf("S1")],
                   SG=[Buf("SG0"), Buf("SG1")], LN=[Buf("LN1"), Buf("LN2")],
                   ACC=[Buf(f"ACC{n}") for n in range(4)])
        R_M = dict(M=[Buf(f"M{c}") for c in range(DC)])
        R_F = dict(AF=[Buf("AF0"), Buf("AF1")], SIL=[Buf("SIL0"), Buf("SIL1")])
        allb = lambda R: [b for v in R.values() for b in v]
        XB = lambda i: MRs(i * TH, TH)
        SQ = lambda i: MRs((4 + i) * TH, TH)
        SQACC = MRs(6 * TH, TH)
        RSTD = MRs(7 * TH, TH)
        CB = lambda i: MRs(i * CBW, CBW)
        PB = lambda i: MRs((2 + i) * CBW, CBW)
        SS = lambda i: MRs((4 + i) * CBW, CBW)
        SG = lambda i: MRs(6 * CBW + i * TH, TH)
        LNA = lambda i: MRs(6 * CBW + 2 * TH + i * T, T)
        AFc = lambda h, i, a=0, b=None: MRb[:, (h * FB + i) * T + a: (h * FB + i) * T + (T if b is None else b)]
        SIL = lambda i: MRs(FB * T + i * T, T)

        def ps0(s, n=None):
            return PS[:, 2 * s, 0: (TP if n is None else n)]

        def ps1(s, a=0, b=P1):
            return PS[:, 2 * s + 1, a:b]

        st = dict(ps=0, w=0, tb=0)

        def next_ps():
            s = st["ps"] % NPS
            st["ps"] += 1
            return s

        def load_w(wap, kc0, KC, m0, MW):
            s = st["w"] % NS
            st["w"] += 1
            assert KC * MW <= SLOT
            dst = WR[:, s, 0: KC * MW].rearrange("p (k m) -> p k m", m=MW)
            src = wap[kc0 * 128:(kc0 + KC) * 128, m0:m0 + MW].rearrange("(k p) m -> p k m", p=128)
            K.emit("pool", lambda e, d=dst, s_=src: e.dma_start(out=d, in_=s_), writes=[Wb[s]], dma=f"w{s}")
            return dst, Wb[s]

        def load_w_pack(parts):
            s = st["w"] % NS
            st["w"] += 1
            o, views, ev = 0, [], None
            for n, (wap, kc0, KC, m0, MW) in enumerate(parts):
                assert o + KC * MW <= SLOT
                dst = WR[:, s, o: o + KC * MW].rearrange("p (k m) -> p k m", m=MW)
                src = wap[kc0 * 128:(kc0 + KC) * 128, m0:m0 + MW].rearrange("(k p) m -> p k m", p=128)
                ev = K.emit("pool", lambda e, d=dst, s_=src: e.dma_start(out=d, in_=s_),
                            writes=[Wb[s]] if n == 0 else (), dma=f"w{s}")
                views.append(dst)
                o += KC * MW
            Wb[s].w = ev
            return views, Wb[s]

        def mm_job(wv, wbuf, mcol, KC, rhs_fn, rbufs, widths, s=None):
            if s is None:
                s = next_ps()
            fns = []
            for pi, wdt in enumerate(widths):
                if wdt == 0:
                    continue
                outp = ps0(s, wdt) if pi == 0 else ps1(s, 0, wdt)
                for k in range(KC):
                    fns.append(lambda e, o=outp, l=wv[:, k, mcol:mcol + 128], r=rhs_fn(k, pi), a=(k == 0),
                               z=(k == KC - 1): e.matmul(o, lhsT=l, rhs=r, start=a, stop=z))
            K.group("pe", fns, reads=[wbuf] + list(rbufs), writes=[PSb[s]])
            return s

        def act(out, in_, func, reads, writes, bias=None, scale=None):
            kw = {}
            if bias is not None:
                kw["bias"] = bias
            if scale is not None:
                kw["scale"] = scale
            return K.emit("act", lambda e: e.activation(out=out, in_=in_, func=func, **kw), reads, writes)

        def tt(out, in0, in1, op, reads, writes, en="dve"):
            return K.emit(en, lambda e: e.tensor_tensor(out=out, in0=in0, in1=in1, op=op), reads, writes)

        def ts(out, in0, s1, s2, op0, op1, reads, writes, en="dve"):
            return K.emit(en, lambda e: e.tensor_scalar(out=out, in0=in0, scalar1=s1, scalar2=s2, op0=op0, op1=op1),
                          reads, writes)

        def stt(out, in0, scalar, in1, op0, op1, reads, writes, en="dve"):
            return K.emit(en, lambda e: e.scalar_tensor_tensor(out=out, in0=in0, scalar=scalar, in1=in1, op0=op0,
                                                                op1=op1), reads, writes)

        def cp(out, in_, reads, writes, en="dve"):
            return K.emit(en, lambda e: e.tensor_copy(out=out, in_=in_), reads, writes)

        def dma(out, in_, sem, reads, writes, en="sp"):
            return K.emit(en, lambda e: e.dma_start(out=out, in_=in_), reads, writes, dma=sem)

        def rsqrt_eps(out, in_, reads, writes):
            act(out, in_, AF.Sqrt, reads, writes, bias=cs("eps", 0, 1))
            K.emit("dve", lambda e: e.reciprocal(out=out, in_=out), writes, writes)

        out_sems = set()

        dma(CST[:, :], cst[:, :], "cst", [], [CSTb])
        K.emit("dve", lambda e: e.memset(ONES_D[:, :], 1.0 / D), [], [ONESb])
        K.emit("dve", lambda e: e.memset(ONES_W[:, :], 1.0 / WC), [], [ONESb])
        act(SCb[:, :], cs("cT", 0, DC * 5), AF.Silu, [CSTb], [SCbb])

        for comp, gname in ((1, "g1"), (4, "g2")):
            bcol = cs("bada", comp * DC, DC)
            stt(bcol, bcol, 1.0, cs(gname, 0, DC), ALU.add, ALU.mult, [CSTb], [CSTb])

        def ada_tile(comp, i):
            wv, wb = load_w(w_ada, 0, DC, comp * D + i * 256, 256)
            slot = next_ps()
            fns = []
            for fl in range(2):
                o = PS[:, 2 * slot, fl * 8: fl * 8 + 5]
                for k in range(DC):
                    fns.append(lambda e, o=o, l=wv[:, k, fl * 128:(fl + 1) * 128], r=SCb[:, k * 5:(k + 1) * 5],
                               a=(k == 0), z=(k == DC - 1): e.matmul(o, lhsT=l, rhs=r, start=a, stop=z))
            K.group("pe", fns, reads=[wb, SCbb], writes=[PSb[slot]])
            for fl in range(2):
                fc = 2 * i + fl
                sc = cs("g1" if comp == 1 else "g2", fc, 1) if comp in (1, 4) else None
                act(MOD[:, comp, :, fc], PS[:, 2 * slot, fl * 8: fl * 8 + 5], AF.Identity, [PSb[slot], CSTb],
                    [MODb[comp]], bias=cs("bada", comp * DC + fc, 1), scale=sc)

        ada_q = [(comp, i) for comp in (2, 3, 4, 5) for i in range(DC // 2)]

        def ada_some(n):
            for _ in range(n):
                if ada_q:
                    ada_tile(*ada_q.pop(0))

        def ada_flush(comp):
            while ada_q and ada_q[0][0] <= comp:
                ada_tile(*ada_q.pop(0))

        def modv(comp, r, c):
            return MOD[:, comp, r, c:c + 1]

        def rms_stats(src, W, w1, mid=None):
            for c in range(DC):
                ap, bufs = src(c)
                act(SQ(c % 2)[:, :W], ap, AF.Square, bufs, [R_norm["SQ"][c % 2]])
                if c == 0:
                    cp(SQACC[:, :W], SQ(0)[:, :W], [R_norm["SQ"][0]], R_norm["ACC"])
                else:
                    tt(SQACC[:, :W], SQACC[:, :W], SQ(c % 2)[:, :W], ALU.add, [R_norm["SQ"][c % 2]] + R_norm["ACC"],
                       R_norm["ACC"])
            if mid is not None:
                mid()
            s = next_ps()
            fns = [lambda e: e.matmul(ps0(s), lhsT=ONES_D[:, :], rhs=SQACC[:, 0:TP], start=True, stop=True),
                   lambda e: e.matmul(ps1(s, 0, w1), lhsT=ONES_D[:, :], rhs=SQACC[:, TP:TP + w1], start=True,
                                      stop=True)]
            K.group("pe", fns, reads=[ONESb] + R_norm["ACC"], writes=[PSb[s]])
            rsqrt_eps(RSTD[:, 0:TP], ps0(s), [PSb[s]], R_norm["RS"])
            rsqrt_eps(RSTD[:, TP:TP + w1], ps1(s, 0, w1), [PSb[s]], R_norm["RS"])

        for t in range(NT):
            rows = (0, 1 + 2 * t, 2 + 2 * t)
            G_U = ((0, TP + 32, rows[0]), (TP + 32, TP + 64, rows[1]), (TP + 64, TH, rows[2]))
            G_M = ((0, TP, rows[0]), (TP, TP + 32, rows[1]), (TP + 32, T, rows[2]))
            ucol = lambda a: a if a < TP else a + 32
            mwid = (TP, SQW)
            u_main = lambda k, pi: U[:, k, 0:TP] if pi == 0 else U[:, k, TP + 32:TH]

            K.switch(allb(R_F), allb(R_norm))

            def ada_first():
                if t == 0:
                    for comp in (0, 1):
                        for i in range(DC // 2):
                            ada_tile(comp, i)

            def xsrc(c):
                i = c % nXB
                dma(XB(i), xT[t, :, c, :], f"xl{i}", [], [R_norm["XB"][i]])
                return XB(i), [R_norm["XB"][i]]

            rms_stats(xsrc, TH, 96, mid=ada_first)
            for c in range(DC):
                xb, xbb = xsrc(c)
                tt(xb, xb, RSTD[:, :TH], ALU.mult, xbb + R_norm["RS"], xbb)
                for (a, b, r) in G_U:
                    act(U[:, c, a:b], xb[:, a:b], AF.Identity, xbb + [MODb[0], MODb[1]], [Ub[c]],
                        bias=modv(0, r, c), scale=modv(1, r, c))

            K.switch(allb(R_norm), allb(R_A))
            hm = cs("hm", t, 1)
            u_all = lambda k, pi: U[:, k, 0:TP] if pi == 0 else U[:, k, TP:TH]
            AW = CBW - 32
            ACC = lambda n: MRs(6 * CBW + 2 * TH + 2 * T + n * AW, AW)
            ACCb = R_A["ACC"]

            def a_tiles(j0):
                return [load_w(w_in, 0, DC, j0 * 128, 256), load_w(w_in, 0, DC, WC + j0 * 128, 256),
                        load_w(w_in, 0, DC, 2 * WC + j0 * 128, 256)]

            pre = a_tiles(0)
            for j0 in range(0, WCC, 2):
                jj = (j0, j0 + 1)
                (wva, wba), (wvg, wbg), (wvp, wbp) = pre
                sza = [mm_job(wva, wba, q * 128, DC, u_all, Ub, (TP, 96)) for q in range(2)]
                szg = [mm_job(wvg, wbg, q * 128, DC, u_all, Ub, (TP, 96)) for q in range(2)]
                for q, j in enumerate(jj):
                    i = j % 2
                    cb, cbb = CB(i), [R_A["CB"][i]]
                    sg, sgb = SG(i), [R_A["SG"][i]]
                    cbs = cb[:, 32 + TP:CBW].rearrange("p (s w) -> p s w", w=64)
                    dma(cbs[:, :, 0:32], stc[t, :, j, :, :], f"cbh{i}", [], cbb)
                    pa, pg = sza[q], szg[q]
                    act(sg[:, 0:TP], ps0(pg), AF.Sigmoid, [PSb[pg]], sgb)
                    act(sg[:, TP:TH], ps1(pg), AF.Sigmoid, [PSb[pg]], sgb)
                    tt(cb[:, 32:32 + TP], ps0(pa), sg[:, 0:TP], ALU.mult, [PSb[pa]] + sgb, cbb)
                    stt(cb[:, 0:32], ps1(pa, 0, 32), hm, sg[:, TP:TP + 32], ALU.mult, ALU.mult, [PSb[pa], CSTb] + sgb,
                        cbb)
                    tt(cbs[:, :, 32:64], ps1(pa, 32, 96).rearrange("p (s w) -> p s w", w=32),
                       sg[:, TP + 32:TH].rearrange("p (s w) -> p s w", w=32), ALU.mult, [PSb[pa]] + sgb, cbb)
                    dma(oc_s[t, :, j, :, :], cbs[:, :, 34:64], f"cbo{i}", cbb, [])
                    dma(oc_p[t, :, j, :], cb[:, 2 + TP:32 + TP], f"cbo{i}", cbb, [])
                    out_sems.add(f"cbo{i}")
                szp = [mm_job(wvp, wbp, q * 128, DC, u_all, Ub, (TP, 96)) for q in range(2)]
                for q, j in enumerate(jj):
                    i = j % 2
                    pp = szp[q]
                    pb, pbb = PB(i), [R_A["PB"][i]]
                    pbs = pb[:, 32 + TP:CBW].rearrange("p (s w) -> p s w", w=64)
                    dma(pbs[:, :, 0:32], stp[t, :, j, :, :], f"pbh{i}", [], pbb)
                    act(pb[:, 32:32 + TP], ps0(pp), AF.Copy, [PSb[pp]], pbb)
                    act(pb[:, 0:32], ps1(pp, 0, 32), AF.Copy, [PSb[pp], CSTb], pbb, scale=hm)
                    act(pbs[:, :, 32:64], ps1(pp, 32, 96).rearrange("p (s w) -> p s w", w=32), AF.Copy, [PSb[pp]], pbb)
                    dma(op_s[t, :, j, :, :], pbs[:, :, 49:64], f"pbo{i}", pbb, [])
                    dma(op_p[t, :, j, :], pb[:, 17 + TP:32 + TP], f"pbo{i}", pbb, [])
                    out_sems.add(f"pbo{i}")
                ada_some(ADA_A)
                if j0 + 2 < WCC:
                    pre = a_tiles(j0 + 2)
                for q, j in enumerate(jj):
                    i = j % 2
                    cb, cbb = CB(i), [R_A["CB"][i]]
                    act(ACC(2 * i), cb[:, 2:2 + AW], AF.Identity, cbb + [CSTb], [ACCb[2 * i]],
                        bias=cs("cb", j, 1), scale=cs("cw", j * CONV_K, 1))
                    act(ACC(2 * i + 1), cb[:, 3:3 + AW], AF.Identity, cbb + [CSTb], [ACCb[2 * i + 1]],
                        scale=cs("cw", j * CONV_K + 1, 1))
                pe_ = "pool" if t >= 1 else "dve"
                for q, j in enumerate(jj):
                    i = j % 2
                    cb, cbb = CB(i), [R_A["CB"][i]]
                    pb, pbb = PB(i), [R_A["PB"][i]]
                    pbs = pb[:, 32 + TP:CBW].rearrange("p (s w) -> p s w", w=64)
                    g = j // GCC
                    src, srcb = pb, pbb
                    for step in range(g + 1):
                        sh = 1 << step
                        lo = (1 << (step + 1)) - 1
                        dst, dstb = SS(step % 2), [R_A["S"][step % 2]]
                        tt(dst[:, lo:CBW], src[:, lo:CBW], src[:, lo - sh:CBW - sh], ALU.add, srcb, dstb, en=pe_)
                        src, srcb = dst, dstb
                    inv_w = 1.0 / POOL_W[g]
                    srs = src[:, 32 + TP:CBW].rearrange("p (s w) -> p s w", w=64)
                    oth, othb = SS((g + 1) % 2), [R_A["S"][(g + 1) % 2]]
                    if pe_ == "dve":
                        stt(Pc(j, 0, TP), src[:, 32:32 + TP], inv_w, pb[:, 32:32 + TP], ALU.mult, ALU.subtract,
                            srcb + pbb, [Pb(j)])
                        stt(Pc(j, TP, T).rearrange("p (s w) -> p s w", w=32), srs[:, :, 32:64], inv_w,
                            pbs[:, :, 32:64], ALU.mult, ALU.subtract, srcb + pbb, [Pb(j)])
                        tt(oth[:, 0:16], src[:, 32:48], cs("invc", (t * 4 + g) * 16, 16), ALU.mult, srcb + [CSTb], othb)
                        tt(Pc(j, 0, 16), oth[:, 0:16], pb[:, 32:48], ALU.subtract, othb + pbb, [Pb(j)])
                    else:
                        tt(oth[:, 0:16], src[:, 32:48], cs("invc", (t * 4 + g) * 16, 16), ALU.mult, srcb + [CSTb], othb,
                           en="pool")
                        ts(src[:, 32:CBW], src[:, 32:CBW], inv_w, 0.0, ALU.mult, ALU.add, srcb, srcb, en="pool")
                        tt(Pc(j, 0, TP), src[:, 32:32 + TP], pb[:, 32:32 + TP], ALU.subtract, srcb + pbb, [Pb(j)],
                           en="pool")
                        tt(Pc(j, TP, T).rearrange("p (s w) -> p s w", w=32), srs[:, :, 32:64], pbs[:, :, 32:64],
                           ALU.subtract, srcb + pbb, [Pb(j)], en="pool")
                        tt(Pc(j, 0, 16), oth[:, 0:16], pb[:, 32:48], ALU.subtract, othb + pbb, [Pb(j)], en="pool")
                    cwv = lambda k, j=j: cs("cw", j * CONV_K + k, 1)
                    for k in range(2, CONV_K):
                        dst, dstb = ACC(2 * i + k % 2), [ACCb[2 * i + k % 2]]
                        srcw = cb[:, 2 + k:2 + k + AW]
                        stt(dst, srcw, cwv(k), dst, ALU.mult, ALU.add, cbb + [CSTb] + dstb, dstb)
                    sv = lambda a_: a_[:, TP:AW].rearrange("p (s w) -> p s w", w=64)[:, :, 32:64]
                    A0, A1, A0b, A1b = ACC(2 * i), ACC(2 * i + 1), ACCb[2 * i], ACCb[2 * i + 1]
                    tt(Hc(j, 0, TP), A0[:, 0:TP], A1[:, 0:TP], ALU.add, [A0b, A1b], [Hb[j]])
                    tt(Hc(j, TP, T).rearrange("p (s w) -> p s w", w=32), sv(A0), sv(A1), ALU.add, [A0b, A1b], [Hb[j]])
                    sg, sgb = SG(i), [R_A["SG"][i]]
                    ln1, ln2 = LNA(0), LNA(1)
                    if j == 0:
                        cp(ln1, Hc(j), [Hb[j]], [R_A["LN"][0]])
                    else:
                        tt(ln1, ln1, Hc(j), ALU.add, [Hb[j], R_A["LN"][0]], [R_A["LN"][0]])
                    act(sg[:, 0:T], Hc(j), AF.Square, [Hb[j]], sgb)
                    if j == 0:
                        cp(ln2, sg[:, 0:T], sgb, [R_A["LN"][1]])
                    else:
                        tt(ln2, ln2, sg[:, 0:T], ALU.add, sgb + [R_A["LN"][1]], [R_A["LN"][1]])

            sm = next_ps()
            K.group("pe", [lambda e, s=sm: e.matmul(ps0(s), lhsT=ONES_W[:, :], rhs=LNA(0)[:, 0:TP], start=True, stop=True),
                           lambda e, s=sm: e.matmul(ps1(s, 0, SQW), lhsT=ONES_W[:, :], rhs=LNA(0)[:, TP:T], start=True,
                                                    stop=True)], reads=[ONESb, R_A["LN"][0]], writes=[PSb[sm]])
            se = next_ps()
            K.group("pe", [lambda e, s=se: e.matmul(ps0(s), lhsT=ONES_W[:, :], rhs=LNA(1)[:, 0:TP], start=True, stop=True),
                           lambda e, s=se: e.matmul(ps1(s, 0, SQW), lhsT=ONES_W[:, :], rhs=LNA(1)[:, TP:T], start=True,
                                                    stop=True)], reads=[ONESb, R_A["LN"][1]], writes=[PSb[se]])
            MEAN, MEANb = SS(0)[:, 0:T], [R_A["S"][0]]
            RSL, RSLb = SS(1)[:, 0:T], [R_A["S"][1]]
            MSQ, MSQb = SG(0)[:, 0:T], [R_A["SG"][0]]
            cp(MEAN[:, 0:TP], ps0(sm), [PSb[sm]], MEANb)
            cp(MEAN[:, TP:T], ps1(sm, 0, SQW), [PSb[sm]], MEANb)
            tt(MSQ, MEAN, MEAN, ALU.mult, MEANb, MSQb)
            tt(RSL[:, 0:TP], ps0(se), MSQ[:, 0:TP], ALU.subtract, [PSb[se]] + MSQb, RSLb)
            tt(RSL[:, TP:T], ps1(se, 0, SQW), MSQ[:, TP:T], ALU.subtract, [PSb[se]] + MSQb, RSLb)
            rsqrt_eps(RSL, RSL, RSLb, RSLb)
            for j in range(WCC):
                tt(Hc(j), Hc(j), MEAN, ALU.subtract, [Hb[j]] + MEANb, [Hb[j]])
                tt(Hc(j), Hc(j), RSL, ALU.mult, [Hb[j]] + RSLb, [Hb[j]])
                act(ACTC(j), Hc(j), AF.Silu, [Hb[j], CSTb], [ACTCb(j)], bias=cs("lnb", j, 1), scale=cs("lng", j, 1))

            K.switch(allb(R_A), allb(R_M))

            def tb_next():
                i = st["tb"] % nTB
                st["tb"] += 1
                return Hc(i), [Hb[i]]

            r_actc = lambda k, pi: ACTC(k, 0, TP) if pi == 0 else ACTC(k, TP, T)
            for d0 in range(0, DC, 2):
                dd = (d0, d0 + 1)
                sa, sbm = [], []
                wv, wb = load_w(w_in, 0, DC, 3 * WC + d0 * 128, 256)
                for q in range(2):
                    s = mm_job(wv, wb, q * 128, DC, u_main, Ub, mwid)
                    tmp, tmpb = tb_next()
                    act(tmp[:, 0:TP], ps0(s), AF.Sigmoid, [PSb[s]], tmpb)
                    act(tmp[:, TP:T], ps1(s, 0, SQW), AF.Sigmoid, [PSb[s]], tmpb)
                    sa.append((tmp, tmpb))
                wv, wb = load_w(w_in, 0, DC, 3 * WC + D + d0 * 128, 256)
                for q in range(2):
                    s = mm_job(wv, wb, q * 128, DC, u_main, Ub, mwid)
                    tmp, tmpb = tb_next()
                    act(tmp[:, 0:TP], ps0(s), AF.Sigmoid, [PSb[s]], tmpb)
                    act(tmp[:, TP:T], ps1(s, 0, SQW), AF.Sigmoid, [PSb[s]], tmpb)
                    sbm.append((tmp, tmpb))
                g = d0 // GOC
                (wv, wvp), wb = load_w_pack([(w_co, 0, WCC, d0 * 128, 256),
                                             (w_pl, g * GCC, GCC, (d0 % GOC) * 128, 256)])
                for q in range(2):
                    s = mm_job(wv, wb, q * 128, WCC, r_actc, [ACTCb(k) for k in range(0, WCC, 2)], mwid)
                    tmp, tmpb = sa[q]
                    tt(tmp[:, 0:TP], ps0(s), tmp[:, 0:TP], ALU.mult, [PSb[s]] + tmpb, tmpb)
                    tt(tmp[:, TP:T], ps1(s, 0, SQW), tmp[:, TP:T], ALU.mult, [PSb[s]] + tmpb, tmpb)
                wv = wvp
                r_p = lambda k, pi, g=g: Pc(g * GCC + k, 0, TP) if pi == 0 else Pc(g * GCC + k, TP, T)
                for q, d in enumerate(dd):
                    s = mm_job(wv, wb, q * 128, GCC, r_p, [Pb(g * GCC + k) for k in range(GCC)], mwid)
                    tmp, tmpb = sbm[q]
                    psc = cs("psc", d, 1)
                    stt(tmp[:, 0:TP], ps0(s), psc, tmp[:, 0:TP], ALU.mult, ALU.mult, [PSb[s], CSTb] + tmpb, tmpb)
                    stt(tmp[:, TP:T], ps1(s, 0, SQW), psc, tmp[:, TP:T], ALU.mult, ALU.mult, [PSb[s], CSTb] + tmpb,
                        tmpb)
                    tt(Mc(d), sa[q][0], tmp, ALU.add, sa[q][1] + tmpb, [R_M["M"][d]])
                ada_some(ADA_B)

            ada_flush(2)
            HR3 = HR[:, :].rearrange("p (c t) -> p c t", t=T)
            for qd in range(DC // 8):
                c0, c1 = qd * 8, qd * 8 + 8
                dma(HR3[:, c0:c1, 0:TP], xT[t, :, c0:c1, 0:TP], f"hl{qd}", [], Hb[c0:c1])
                dma(HR3[:, c0:c1, TP:T], xT[t, :, c0:c1, TP + 32:TH], f"hl{qd}", [], Hb[c0:c1])

            r_m = lambda k, pi: Mc(k, 0, TP) if pi == 0 else Mc(k, TP, T)
            for d0 in range(0, DC, 2):
                wv, wb = load_w(w_out, 0, DC, d0 * 128, 256)
                for q in range(2):
                    d = d0 + q
                    s = mm_job(wv, wb, q * 128, DC, r_m, R_M["M"], mwid)
                    for (a, b, r) in G_M:
                        src = ps0(s)[:, a:b] if a < TP else ps1(s, a - TP, b - TP)
                        stt(Hc(d, a, b), src, modv(2, r, d), Hc(d, a, b), ALU.mult, ALU.add, [PSb[s], MODb[2], Hb[d]],
                            [Hb[d]])

            ada_flush(4)
            K.switch(allb(R_M), allb(R_norm))
            rms_stats(lambda c: (Hc(c), [Hb[c]]), T, SQW)
            for c in range(DC):
                tmp, tmpb = SQ(c % 2)[:, :T], [R_norm["SQ"][c % 2]]
                tt(tmp, Hc(c), RSTD[:, :T], ALU.mult, [Hb[c]] + R_norm["RS"], tmpb)
                for (a, b, r) in G_M:
                    act(U[:, c, ucol(a):ucol(a) + (b - a)], tmp[:, a:b], AF.Identity, tmpb + [MODb[3], MODb[4]],
                        [Ub[c]], bias=modv(3, r, c), scale=modv(4, r, c))

            ada_flush(5)
            K.switch(allb(R_norm), allb(R_F))
            nblk = (FC + FB - 1) // FB
            for bi in range(nblk):
                f0, f1 = bi * FB, min(FC, (bi + 1) * FB)
                nb = f1 - f0
                hb = bi % 2
                afb = [R_F["AF"][hb]]
                i = f0
                while i < f1:
                    n2 = 2 if i + 1 < f1 else 1
                    wv, wb = load_w(w_fi, 0, DC, i * 128, 128 * n2)
                    sgs = [mm_job(wv, wb, q * 128, DC, u_main, Ub, mwid) for q in range(n2)]
                    for q in range(n2):
                        s = sgs[q]
                        act(SIL(q)[:, 0:TP], ps0(s), AF.Silu, [PSb[s]], [R_F["SIL"][q]])
                        act(SIL(q)[:, TP:T], ps1(s, 0, SQW), AF.Silu, [PSb[s]], [R_F["SIL"][q]])
                    wv, wb = load_w(w_fi, 0, DC, DFF + i * 128, 128 * n2)
                    sus = [mm_job(wv, wb, q * 128, DC, u_main, Ub, mwid) for q in range(n2)]
                    for q in range(n2):
                        s = sus[q]
                        li = i + q - f0
                        tt(AFc(hb, li, 0, TP), SIL(q)[:, 0:TP], ps0(s), ALU.mult, [PSb[s], R_F["SIL"][q]], afb)
                        tt(AFc(hb, li, TP, T), SIL(q)[:, TP:T], ps1(s, 0, SQW), ALU.mult, [PSb[s], R_F["SIL"][q]], afb)
                    i += n2
                r_af = lambda k, pi, hb=hb: AFc(hb, k, 0, TP) if pi == 0 else AFc(hb, k, TP, T)
                for d0 in range(0, DC, 4):
                    wv, wb = load_w(w_fo, f0, nb, d0 * 128, 512)
                    for q in range(4):
                        d = d0 + q
                        s = mm_job(wv, wb, q * 128, nb, r_af, afb, mwid)
                        for (a, b, r) in G_M:
                            src = ps0(s)[:, a:b] if a < TP else ps1(s, a - TP, b - TP)
                            stt(Hc(d, a, b), src, modv(5, r, d), Hc(d, a, b), ALU.mult, ALU.add,
                                [PSb[s], MODb[5], Hb[d]], [Hb[d]])

            K.switch(allb(R_F), allb(R_norm))
            rms_stats(lambda c: (Hc(c), [Hb[c]]), T, SQW)
            for qd in range(DC // 8):
                c0, c1 = qd * 8, qd * 8 + 8
                for c in range(c0, c1):
                    stt(Hc(c), Hc(c), cs("gf", c, 1), RSTD[:, :T], ALU.mult, ALU.mult, [Hb[c], CSTb] + R_norm["RS"],
                        [Hb[c]])
                dma(yT[t, :, c0:c1, :], HR3[:, c0:c1, :], f"ys{qd}", Hb[c0:c1], [])
                out_sems.add(f"ys{qd}")
            K.switch(allb(R_norm), allb(R_F))

        final_waits = [(k, K.dmacnt[k]) for k in sorted(out_sems)]
        K.eng["sp"].q.append((final_waits, None, None))

        keys = ["pe", "act", "dve", "pool"] + sorted(K.dmacnt.keys())
        sems = {k: es.enter_context(nc.semaphore(k)) for k in keys}

        def replay(name):
            def run(e):
                for waits, fn, inc in K.eng[name].q:
                    for (k, v) in waits:
                        e.wait_ge(sems[k], v)
                    if fn is None:
                        continue
                    ins = fn(e)
                    if inc is not None:
                        ins.then_inc(sems[inc[0]], inc[1])
            return run

        with nc.Block() as block:
            block.sync(replay("sp"))
            block.gpsimd(replay("pool"))
            block.tensor(replay("pe"))
            block.scalar(replay("act"))
            block.vector(replay("dve"))
    info = dict(DC=DC, WCC=WCC, T=T, TH=TH, off=off, NCST=NCST, FB=FB,
                n_instr={k: len(v.q) for k, v in K.eng.items()})
    return nc, info


_CACHE = {}


def _host_inputs(D, DFF, TP, NT, inp):
    DC, WC = D // 128, D // 2
    WCC = WC // 128
    T, TH = TP + 64, TP + 96
    f32 = np.float32
    xp = np.asarray(inp["x_prompt"], f32)[0]
    xs = np.asarray(inp["x_sample"], f32)
    sc = np.asarray(inp["state_conv"], f32)[0]
    spl = np.asarray(inp["state_pool"], f32)[0]
    fm = lambda v, n: np.ascontiguousarray(np.asarray(v, f32).reshape(n, 128).T)
    shared = dict(
        w_ada=np.ascontiguousarray(np.asarray(inp["w_ada"], f32)[0]),
        w_in=np.ascontiguousarray(np.asarray(inp["w_in"], f32)[0]),
        w_co=np.ascontiguousarray(np.asarray(inp["w_conv_out"], f32)[0]),
        w_pl=np.ascontiguousarray(np.asarray(inp["w_pool"], f32)[0].reshape(-1, D // 4)),
        w_out=np.ascontiguousarray(np.asarray(inp["w_out"], f32)[0]),
        w_fi=np.ascontiguousarray(np.asarray(inp["w_ffn_in"], f32)[0]),
        w_fo=np.ascontiguousarray(np.asarray(inp["w_ffn_out"], f32)[0]),
    )
    bada = fm(inp["b_ada"][0], 6 * DC)
    g1, g2, gf = fm(inp["g_norm1"][0], DC), fm(inp["g_norm2"][0], DC), fm(inp["g_final"], DC)
    psc = fm(inp["pool_scale"][0], DC)
    cw = np.asarray(inp["conv_w"], f32)[0]
    cwT = np.ascontiguousarray(cw.T.reshape(WCC, 128, CONV_K).transpose(1, 0, 2)).reshape(128, WCC * CONV_K)
    cb, lng, lnb = fm(inp["conv_b"][0], WCC), fm(inp["ln_g"][0], WCC), fm(inp["ln_b"][0], WCC)
    maps = []
    for c in range(NCORES):
        xt = np.zeros((NT, TH, D), f32)
        hm = np.ones((128, NT), f32)
        invc = np.zeros((128, NT, 4, 16), f32)
        stc = np.zeros((NT, 128, WCC, 2, 32), f32)
        stp = np.zeros((NT, 128, WCC, 2, 32), f32)
        for t in range(NT):
            start = (c * NT + t) * TP
            xt[t, 0:TP] = xp[start:start + TP]
            if start >= 32:
                xt[t, TP:TP + 32] = xp[start - 32:start]
            else:
                hm[:, t] = 0.0
            for s in range(2):
                q = c * 2 * NT + 2 * t + s
                xt[t, TP + 32 + 32 * s:TP + 64 + 32 * s] = xs[q]
                stc[t, :, :, s, 2:32] = sc[q].T.reshape(WCC, 128, 30).transpose(1, 0, 2)
                stp[t, :, :, s, 17:32] = spl[q].T.reshape(WCC, 128, 15).transpose(1, 0, 2)
            pos = start + np.arange(16)
            for g, w in enumerate(POOL_W):
                invc[:, t, g, :] = (1.0 / np.minimum(float(w), pos + 1.0)).astype(f32)[None, :]
        xTc = np.ascontiguousarray(xt.reshape(NT, TH, DC, 128).transpose(0, 3, 2, 1))
        crow = np.concatenate([np.asarray(inp["c_prompt"], f32), np.asarray(inp["c_sample"], f32)[4 * c:4 * c + 4]]
                              if NT == 2 else [np.asarray(inp["c_prompt"], f32)], axis=0)
        cT = np.ascontiguousarray(crow.reshape(5, DC, 128).transpose(2, 1, 0)).reshape(128, DC * 5)
        cst = np.concatenate([cT, bada, g1, g2, gf, psc, cwT, cb, lng, lnb, hm, invc.reshape(128, -1),
                              np.full((128, 1), EPS, f32)], axis=1)
        m = dict(xT=xTc, cst=np.ascontiguousarray(cst.astype(f32)), stc=stc, stp=stp)
        m.update(shared)
        maps.append(m)
    return maps


def kernel(**inp):
    D = inp["x_prompt"].shape[2]
    SEQ = inp["x_prompt"].shape[1]
    DFF = inp["w_ffn_out"].shape[1]
    NT = 2
    TP = SEQ // (NCORES * NT)
    NSEQ = inp["x_sample"].shape[0]
    assert NSEQ == NCORES * 2 * NT and inp["x_sample"].shape[1] == 32
    key = (D, DFF, TP, NT)
    if key not in _CACHE:
        _CACHE[key] = build(D, DFF, TP, NT)
    nc, info = _CACHE[key]
    maps = _host_inputs(D, DFF, TP, NT, inp)
    assert maps[0]["cst"].shape[1] == info["NCST"]
    res = run_bass_kernel_spmd(nc, maps, core_ids=list(range(NCORES)))
    DC, WC = D // 128, D // 2
    WCC = WC // 128
    f32 = np.float32
    y_p = np.zeros((1, SEQ, D), f32)
    y_s = np.zeros((NSEQ, 32, D), f32)
    ncs = np.zeros((1, NSEQ, 30, WC), f32)
    nps = np.zeros((1, NSEQ, 15, WC), f32)
    for c in range(NCORES):
        r = res.results[c]
        yT = np.asarray(r["yT"])
        y = yT.transpose(0, 3, 2, 1).reshape(NT, -1, D)
        ocs = np.asarray(r["oc_s"])
        ops = np.asarray(r["op_s"])
        for t in range(NT):
            start = (c * NT + t) * TP
            y_p[0, start:start + TP] = y[t, 0:TP]
            for s in range(2):
                q = c * 2 * NT + 2 * t + s
                y_s[q] = y[t, TP + 32 * s:TP + 32 * s + 32]
                ncs[0, q] = ocs[t, :, :, s, :].transpose(2, 1, 0).reshape(30, WC)
                nps[0, q] = ops[t, :, :, s, :].transpose(2, 1, 0).reshape(15, WC)
    last = res.results[NCORES - 1]
    ncp = np.asarray(last["oc_p"])[NT - 1].transpose(2, 1, 0).reshape(1, 1, 30, WC).astype(f32)
    npp = np.asarray(last["op_p"])[NT - 1].transpose(2, 1, 0).reshape(1, 1, 15, WC).astype(f32)
    return (y_p, y_s, np.ascontiguousarray(ncp), np.ascontiguousarray(npp), ncs, nps)
```

```python
import contextlib
import numpy as np
import concourse.bass as bass
import concourse.mybir as mybir
from concourse.bass_utils import run_bass_kernel_spmd

F32 = mybir.dt.float32
BF16 = mybir.dt.bfloat16
AF = mybir.ActivationFunctionType
ALU = mybir.AluOpType

NCORES = 8
CONV_K = 31
POOL_W = (2, 4, 8, 16)
EPS = 1e-6
NS = 3
SLOT = 8192
NPS = 4
ADA_A = 5
ADA_B = 2


class Buf:
    __slots__ = ("name", "w", "r")

    def __init__(self, name):
        self.name = name
        self.w = None
        self.r = {}


class Eng:
    def __init__(self, name):
        self.name = name
        self.q = []
        self.cnt = 0
        self.seen = {}


class Ctx:
    def __init__(self):
        self.eng = {n: Eng(n) for n in ("pe", "act", "dve", "pool", "sp")}
        self.dmacnt = {}

    def emit(self, en, fn, reads=(), writes=(), sig=True, dma=None):
        eng = self.eng[en]
        need = {}

        def add(k, v):
            if need.get(k, 0) < v:
                need[k] = v

        for b in reads:
            if b.w is not None:
                add(*b.w)
        for b in writes:
            if b.w is not None:
                add(*b.w)
            for k, v in b.r.items():
                add(k, v)
        waits = []
        for k, v in need.items():
            if eng.seen.get(k, 0) < v:
                eng.seen[k] = v
                waits.append((k, v))
        if dma is not None:
            self.dmacnt[dma] = self.dmacnt.get(dma, 0) + 16
            ev = (dma, self.dmacnt[dma])
            inc = (dma, 16)
        elif sig:
            eng.cnt += 1
            ev = (en, eng.cnt)
            inc = (en, 1)
        else:
            ev = None
            inc = None
        eng.q.append((waits, fn, inc))
        if ev is not None:
            for b in writes:
                b.w = ev
                b.r = {}
            for b in reads:
                if b.r.get(ev[0], 0) < ev[1]:
                    b.r[ev[0]] = ev[1]
        return ev

    def group(self, en, fns, reads=(), writes=()):
        n = len(fns)
        for i, fn in enumerate(fns):
            last = i == n - 1
            if i == 0 and not last:
                self.emit(en, fn, reads, writes, sig=False)
            elif last:
                self.emit(en, fn, reads, writes, sig=True)
            else:
                self.eng[en].q.append(((), fn, None))

    @staticmethod
    def switch(old, new):
        ev = {}
        for b in old:
            if b.w is not None and ev.get(b.w[0], 0) < b.w[1]:
                ev[b.w[0]] = b.w[1]
            for k, v in b.r.items():
                if ev.get(k, 0) < v:
                    ev[k] = v
        for b in new:
            b.w = None
            b.r = dict(ev)


def build(D, DFF, TP, NT):
    DC = D // 128
    WC = D // 2
    WCC = WC // 128
    GC = WC // 4
    GCC = GC // 128
    GO = D // 4
    GOC = GO // 128
    FC = DFF // 128
    DIN = 3 * WC + 2 * D
    SQW = 64
    T = TP + SQW
    TH = T + 32
    CBW = 32 + TP + 128
    P1 = 96
    assert TP <= 512 and DC * 5 <= 160 and WCC % 2 == 0 and DC % 8 == 0

    MRW = max(DC * T // 2, 6 * CBW + 2 * TH + 2 * T + 4 * (CBW - 32), 8 * TH)
    FB = min(FC, (MRW - 2 * T) // T)
    if FB < FC:
        pass
    MRW = max(MRW, min(FB, FC) * T + 2 * T)
    assert FB * 512 <= SLOT and DC * 256 <= SLOT

    off = {}
    o = 0
    for name, n in (("cT", DC * 5), ("bada", 6 * DC), ("g1", DC), ("g2", DC), ("gf", DC), ("psc", DC),
                    ("cw", WCC * CONV_K), ("cb", WCC), ("lng", WCC), ("lnb", WCC),
                    ("hm", NT), ("invc", NT * 4 * 16), ("eps", 1)):
        off[name] = o
        o += n
    NCST = o

    nc = bass.Bass("TRN2", target_bir_lowering=False)
    dt = nc.dram_tensor
    xT = dt("xT", [NT, 128, DC, TH], F32, kind="ExternalInput").ap()
    cst = dt("cst", [128, NCST], F32, kind="ExternalInput").ap()
    stc = dt("stc", [NT, 128, WCC, 2, 32], F32, kind="ExternalInput").ap()
    stp = dt("stp", [NT, 128, WCC, 2, 32], F32, kind="ExternalInput").ap()
    w_ada = dt("w_ada", [D, 6 * D], F32, kind="ExternalInput").ap()
    w_in = dt("w_in", [D, DIN], F32, kind="ExternalInput").ap()
    w_co = dt("w_co", [WC, D], F32, kind="ExternalInput").ap()
    w_pl = dt("w_pl", [4 * GC, GO], F32, kind="ExternalInput").ap()
    w_out = dt("w_out", [D, D], F32, kind="ExternalInput").ap()
    w_fi = dt("w_fi", [D, 2 * DFF], F32, kind="ExternalInput").ap()
    w_fo = dt("w_fo", [DFF, D], F32, kind="ExternalInput").ap()
    yT = dt("yT", [NT, 128, DC, T], F32, kind="ExternalOutput").ap()
    oc_s = dt("oc_s", [NT, 128, WCC, 2, 30], F32, kind="ExternalOutput").ap()
    op_s = dt("op_s", [NT, 128, WCC, 2, 15], F32, kind="ExternalOutput").ap()
    oc_p = dt("oc_p", [NT, 128, WCC, 30], F32, kind="ExternalOutput").ap()
    op_p = dt("op_p", [NT, 128, WCC, 15], F32, kind="ExternalOutput").ap()

    K = Ctx()
    with contextlib.ExitStack() as es:
        sb = lambda name, shape, dtype: es.enter_context(nc.sbuf_tensor(name, shape, dtype))
        CST = sb("CST", [128, NCST], F32)
        MOD = sb("MOD", [128, 6, 5, DC], F32)
        ONES_D = sb("ONES_D", [128, 128], F32)
        ONES_W = sb("ONES_W", [128, 128], F32)
        SCb = sb("SCb", [128, DC * 5], BF16)
        U = sb("U", [128, DC, TH], BF16)
        HR = sb("HR", [128, DC * T], F32)
        MR = sb("MR", [128, MRW], F32)
        WR = sb("WR", [128, NS, SLOT], BF16)
        PS = es.enter_context(nc.psum_tensor("PS", [128, 8, 512], F32))

        def cs(name, i=0, n=1):
            return CST[:, off[name] + i: off[name] + i + n]

        HRb = HR[:, WCC * T: 2 * WCC * T].bitcast(BF16)
        MRb = MR[:, :].bitcast(BF16)

        def Hc(c, a=0, b=None):
            b = T if b is None else b
            return HR[:, c * T + a: c * T + b]

        def ACTC(j, a=0, b=None):
            b = T if b is None else b
            return HRb[:, j * T + a: j * T + b]

        def Pc(j, a=0, b=None):
            b = T if b is None else b
            return HRb[:, (WCC + j) * T + a: (WCC + j) * T + b]

        def Mc(c, a=0, b=None):
            b = T if b is None else b
            return MRb[:, c * T + a: c * T + b]

        def MRs(o0, n):
            return MR[:, o0: o0 + n]

        CSTb, MODb = Buf("cst"), [Buf(f"mod{i}") for i in range(6)]
        ONESb, SCbb = Buf("ones"), Buf("scb")
        Ub = [Buf(f"U{c}") for c in range(DC)]
        Hb = [Buf(f"H{c}") for c in range(DC)]
        Wb = [Buf(f"W{s}") for s in range(NS)]
        PSb = [Buf(f"PS{s}") for s in range(NPS)]
        ACTCb = lambda j: Hb[WCC + j // 2]
        Pb = lambda j: Hb[WCC + WCC // 2 + j // 2]
        nXB, nTB = 4, min(8, WCC)
        R_norm = dict(XB=[Buf(f"XB{i}") for i in range(nXB)], SQ=[Buf("SQ0"), Buf("SQ1")],
                      ACC=[Buf("SQACC")], RS=[Buf("RSTD")])
        R_A = dict(CB=[Buf("CB0"), Buf("CB1")], PB=[Buf("PB0"), Buf("PB1")], S=[Buf("S0"), Buf("S1")],
                   SG=[Buf("SG0"), Buf("SG1")], LN=[Buf("LN1"), Buf("LN2")],
                   ACC=[Buf(f"ACC{n}") for n in range(4)])
        R_M = dict(M=[Buf(f"M{c}") for c in range(DC)])
        R_F = dict(AF=[Buf("AF0"), Buf("AF1")], SIL=[Buf("SIL0"), Buf("SIL1")])
        allb = lambda R: [b for v in R.values() for b in v]
        XB = lambda i: MRs(i * TH, TH)
        SQ = lambda i: MRs((4 + i) * TH, TH)
        SQACC = MRs(6 * TH, TH)
        RSTD = MRs(7 * TH, TH)
        CB = lambda i: MRs(i * CBW, CBW)
        PB = lambda i: MRs((2 + i) * CBW, CBW)
        SS = lambda i: MRs((4 + i) * CBW, CBW)
        SG = lambda i: MRs(6 * CBW + i * TH, TH)
        LNA = lambda i: MRs(6 * CBW + 2 * TH + i * T, T)
        AFc = lambda h, i, a=0, b=None: MRb[:, (h * FB + i) * T + a: (h * FB + i) * T + (T if b is None else b)]
        SIL = lambda i: MRs(FB * T + i * T, T)

        def ps0(s, n=None):
            return PS[:, 2 * s, 0: (TP if n is None else n)]

        def ps1(s, a=0, b=P1):
            return PS[:, 2 * s + 1, a:b]

        st = dict(ps=0, w=0, tb=0)

        def next_ps():
            s = st["ps"] % NPS
            st["ps"] += 1
            return s

        def load_w(wap, kc0, KC, m0, MW):
            s = st["w"] % NS
            st["w"] += 1
            assert KC * MW <= SLOT
            dst = WR[:, s, 0: KC * MW].rearrange("p (k m) -> p k m", m=MW)
            src = wap[kc0 * 128:(kc0 + KC) * 128, m0:m0 + MW].rearrange("(k p) m -> p k m", p=128)
            K.emit("pool", lambda e, d=dst, s_=src: e.dma_start(out=d, in_=s_), writes=[Wb[s]], dma=f"w{s}")
            return dst, Wb[s]

        def load_w_pack(parts):
            s = st["w"] % NS
            st["w"] += 1
            o, views, ev = 0, [], None
            for n, (wap, kc0, KC, m0, MW) in enumerate(parts):
                assert o + KC * MW <= SLOT
                dst = WR[:, s, o: o + KC * MW].rearrange("p (k m) -> p k m", m=MW)
                src = wap[kc0 * 128:(kc0 + KC) * 128, m0:m0 + MW].rearrange("(k p) m -> p k m", p=128)
                ev = K.emit("pool", lambda e, d=dst, s_=src: e.dma_start(out=d, in_=s_),
                            writes=[Wb[s]] if n == 0 else (), dma=f"w{s}")
                views.append(dst)
                o += KC * MW
            Wb[s].w = ev
            return views, Wb[s]

        def mm_job(wv, wbuf, mcol, KC, rhs_fn, rbufs, widths, s=None):
            if s is None:
                s = next_ps()
            fns = []
            for pi, wdt in enumerate(widths):
                if wdt == 0:
                    continue
                outp = ps0(s, wdt) if pi == 0 else ps1(s, 0, wdt)
                for k in range(KC):
                    fns.append(lambda e, o=outp, l=wv[:, k, mcol:mcol + 128], r=rhs_fn(k, pi), a=(k == 0),
                               z=(k == KC - 1): e.matmul(o, lhsT=l, rhs=r, start=a, stop=z))
            K.group("pe", fns, reads=[wbuf] + list(rbufs), writes=[PSb[s]])
            return s

        def act(out, in_, func, reads, writes, bias=None, scale=None):
            kw = {}
            if bias is not None:
                kw["bias"] = bias
            if scale is not None:
                kw["scale"] = scale
            return K.emit("act", lambda e: e.activation(out=out, in_=in_, func=func, **kw), reads, writes)

        def tt(out, in0, in1, op, reads, writes, en="dve"):
            return K.emit(en, lambda e: e.tensor_tensor(out=out, in0=in0, in1=in1, op=op), reads, writes)

        def ts(out, in0, s1, s2, op0, op1, reads, writes, en="dve"):
            return K.emit(en, lambda e: e.tensor_scalar(out=out, in0=in0, scalar1=s1, scalar2=s2, op0=op0, op1=op1),
                          reads, writes)

        def stt(out, in0, scalar, in1, op0, op1, reads, writes, en="dve"):
            return K.emit(en, lambda e: e.scalar_tensor_tensor(out=out, in0=in0, scalar=scalar, in1=in1, op0=op0,
                                                                op1=op1), reads, writes)

        def cp(out, in_, reads, writes, en="dve"):
            return K.emit(en, lambda e: e.tensor_copy(out=out, in_=in_), reads, writes)

        def dma(out, in_, sem, reads, writes, en="sp"):
            return K.emit(en, lambda e: e.dma_start(out=out, in_=in_), reads, writes, dma=sem)

        def rsqrt_eps(out, in_, reads, writes):
            act(out, in_, AF.Sqrt, reads, writes, bias=cs("eps", 0, 1))
            K.emit("dve", lambda e: e.reciprocal(out=out, in_=out), writes, writes)

        out_sems = set()

        dma(CST[:, :], cst[:, :], "cst", [], [CSTb])
        K.emit("dve", lambda e: e.memset(ONES_D[:, :], 1.0 / D), [], [ONESb])
        K.emit("dve", lambda e: e.memset(ONES_W[:, :], 1.0 / WC), [], [ONESb])
        act(SCb[:, :], cs("cT", 0, DC * 5), AF.Silu, [CSTb], [SCbb])

        for comp, gname in ((1, "g1"), (4, "g2")):
            bcol = cs("bada", comp * DC, DC)
            stt(bcol, bcol, 1.0, cs(gname, 0, DC), ALU.add, ALU.mult, [CSTb], [CSTb])

        def ada_tile(comp, i):
            wv, wb = load_w(w_ada, 0, DC, comp * D + i * 256, 256)
            slot = next_ps()
            fns = []
            for fl in range(2):
                o = PS[:, 2 * slot, fl * 8: fl * 8 + 5]
                for k in range(DC):
                    fns.append(lambda e, o=o, l=wv[:, k, fl * 128:(fl + 1) * 128], r=SCb[:, k * 5:(k + 1) * 5],
                               a=(k == 0), z=(k == DC - 1): e.matmul(o, lhsT=l, rhs=r, start=a, stop=z))
            K.group("pe", fns, reads=[wb, SCbb], writes=[PSb[slot]])
            for fl in range(2):
                fc = 2 * i + fl
                sc = cs("g1" if comp == 1 else "g2", fc, 1) if comp in (1, 4) else None
                act(MOD[:, comp, :, fc], PS[:, 2 * slot, fl * 8: fl * 8 + 5], AF.Identity, [PSb[slot], CSTb],
                    [MODb[comp]], bias=cs("bada", comp * DC + fc, 1), scale=sc)

        ada_q = [(comp, i) for comp in (2, 3, 4, 5) for i in range(DC // 2)]

        def ada_some(n):
            for _ in range(n):
                if ada_q:
                    ada_tile(*ada_q.pop(0))

        def ada_flush(comp):
            while ada_q and ada_q[0][0] <= comp:
                ada_tile(*ada_q.pop(0))

        def modv(comp, r, c):
            return MOD[:, comp, r, c:c + 1]

        def rms_stats(src, W, w1, mid=None):
            for c in range(DC):
                ap, bufs = src(c)
                act(SQ(c % 2)[:, :W], ap, AF.Square, bufs, [R_norm["SQ"][c % 2]])
                if c == 0:
                    cp(SQACC[:, :W], SQ(0)[:, :W], [R_norm["SQ"][0]], R_norm["ACC"])
                else:
                    tt(SQACC[:, :W], SQACC[:, :W], SQ(c % 2)[:, :W], ALU.add, [R_norm["SQ"][c % 2]] + R_norm["ACC"],
                       R_norm["ACC"])
            if mid is not None:
                mid()
            s = next_ps()
            fns = [lambda e: e.matmul(ps0(s), lhsT=ONES_D[:, :], rhs=SQACC[:, 0:TP], start=True, stop=True),
                   lambda e: e.matmul(ps1(s, 0, w1), lhsT=ONES_D[:, :], rhs=SQACC[:, TP:TP + w1], start=True,
                                      stop=True)]
            K.group("pe", fns, reads=[ONESb] + R_norm["ACC"], writes=[PSb[s]])
            rsqrt_eps(RSTD[:, 0:TP], ps0(s), [PSb[s]], R_norm["RS"])
            rsqrt_eps(RSTD[:, TP:TP + w1], ps1(s, 0, w1), [PSb[s]], R_norm["RS"])

        for t in range(NT):
            rows = (0, 1 + 2 * t, 2 + 2 * t)
            G_U = ((0, TP + 32, rows[0]), (TP + 32, TP + 64, rows[1]), (TP + 64, TH, rows[2]))
            G_M = ((0, TP, rows[0]), (TP, TP + 32, rows[1]), (TP + 32, T, rows[2]))
            ucol = lambda a: a if a < TP else a + 32
            mwid = (TP, SQW)
            u_main = lambda k, pi: U[:, k, 0:TP] if pi == 0 else U[:, k, TP + 32:TH]

            K.switch(allb(R_F), allb(R_norm))

            def ada_first():
                if t == 0:
                    for comp in (0, 1):
                        for i in range(DC // 2):
                            ada_tile(comp, i)

            def xsrc(c):
                i = c % nXB
                dma(XB(i), xT[t, :, c, :], f"xl{i}", [], [R_norm["XB"][i]])
                return XB(i), [R_norm["XB"][i]]

            rms_stats(xsrc, TH, 96, mid=ada_first)
            for c in range(DC):
                xb, xbb = xsrc(c)
                tt(xb, xb, RSTD[:, :TH], ALU.mult, xbb + R_norm["RS"], xbb)
                for (a, b, r) in G_U:
                    act(U[:, c, a:b], xb[:, a:b], AF.Identity, xbb + [MODb[0], MODb[1]], [Ub[c]],
                        bias=modv(0, r, c), scale=modv(1, r, c))

            K.switch(allb(R_norm), allb(R_A))
            hm = cs("hm", t, 1)
            u_all = lambda k, pi: U[:, k, 0:TP] if pi == 0 else U[:, k, TP:TH]
            AW = CBW - 32
            ACC = lambda n: MRs(6 * CBW + 2 * TH + 2 * T + n * AW, AW)
            ACCb = R_A["ACC"]

            def a_tiles(j0):
                return [load_w(w_in, 0, DC, j0 * 128, 256), load_w(w_in, 0, DC, WC + j0 * 128, 256),
                        load_w(w_in, 0, DC, 2 * WC + j0 * 128, 256)]

            pre = a_tiles(0)
            for j0 in range(0, WCC, 2):
                jj = (j0, j0 + 1)
                (wva, wba), (wvg, wbg), (wvp, wbp) = pre
                sza = [mm_job(wva, wba, q * 128, DC, u_all, Ub, (TP, 96)) for q in range(2)]
                szg = [mm_job(wvg, wbg, q * 128, DC, u_all, Ub, (TP, 96)) for q in range(2)]
                for q, j in enumerate(jj):
                    i = j % 2
                    cb, cbb = CB(i), [R_A["CB"][i]]
                    sg, sgb = SG(i), [R_A["SG"][i]]
                    cbs = cb[:, 32 + TP:CBW].rearrange("p (s w) -> p s w", w=64)
                    dma(cbs[:, :, 0:32], stc[t, :, j, :, :], f"cbh{i}", [], cbb)
                    pa, pg = sza[q], szg[q]
                    act(sg[:, 0:TP], ps0(pg), AF.Sigmoid, [PSb[pg]], sgb)
                    act(sg[:, TP:TH], ps1(pg), AF.Sigmoid, [PSb[pg]], sgb)
                    tt(cb[:, 32:32 + TP], ps0(pa), sg[:, 0:TP], ALU.mult, [PSb[pa]] + sgb, cbb)
                    stt(cb[:, 0:32], ps1(pa, 0, 32), hm, sg[:, TP:TP + 32], ALU.mult, ALU.mult, [PSb[pa], CSTb] + sgb,
                        cbb)
                    tt(cbs[:, :, 32:64], ps1(pa, 32, 96).rearrange("p (s w) -> p s w", w=32),
                       sg[:, TP + 32:TH].rearrange("p (s w) -> p s w", w=32), ALU.mult, [PSb[pa]] + sgb, cbb)
                    dma(oc_s[t, :, j, :, :], cbs[:, :, 34:64], f"cbo{i}", cbb, [])
                    dma(oc_p[t, :, j, :], cb[:, 2 + TP:32 + TP], f"cbo{i}", cbb, [])
                    out_sems.add(f"cbo{i}")
                szp = [mm_job(wvp, wbp, q * 128, DC, u_all, Ub, (TP, 96)) for q in range(2)]
                for q, j in enumerate(jj):
                    i = j % 2
                    pp = szp[q]
                    pb, pbb = PB(i), [R_A["PB"][i]]
                    pbs = pb[:, 32 + TP:CBW].rearrange("p (s w) -> p s w", w=64)
                    dma(pbs[:, :, 0:32], stp[t, :, j, :, :], f"pbh{i}", [], pbb)
                    act(pb[:, 32:32 + TP], ps0(pp), AF.Copy, [PSb[pp]], pbb)
                    act(pb[:, 0:32], ps1(pp, 0, 32), AF.Copy, [PSb[pp], CSTb], pbb, scale=hm)
                    act(pbs[:, :, 32:64], ps1(pp, 32, 96).rearrange("p (s w) -> p s w", w=32), AF.Copy, [PSb[pp]], pbb)
                    dma(op_s[t, :, j, :, :], pbs[:, :, 49:64], f"pbo{i}", pbb, [])
                    dma(op_p[t, :, j, :], pb[:, 17 + TP:32 + TP], f"pbo{i}", pbb, [])
                    out_sems.add(f"pbo{i}")
                ada_some(ADA_A)
                if j0 + 2 < WCC:
                    pre = a_tiles(j0 + 2)
                for q, j in enumerate(jj):
                    i = j % 2
                    cb, cbb = CB(i), [R_A["CB"][i]]
                    act(ACC(2 * i), cb[:, 2:2 + AW], AF.Identity, cbb + [CSTb], [ACCb[2 * i]],
                        bias=cs("cb", j, 1), scale=cs("cw", j * CONV_K, 1))
                    act(ACC(2 * i + 1), cb[:, 3:3 + AW], AF.Identity, cbb + [CSTb], [ACCb[2 * i + 1]],
                        scale=cs("cw", j * CONV_K + 1, 1))
                pe_ = "dve"
                for q, j in enumerate(jj):
                    i = j % 2
                    cb, cbb = CB(i), [R_A["CB"][i]]
                    pb, pbb = PB(i), [R_A["PB"][i]]
                    pbs = pb[:, 32 + TP:CBW].rearrange("p (s w) -> p s w", w=64)
                    g = j // GCC
                    src, srcb = pb, pbb
                    for step in range(g + 1):
                        sh = 1 << step
                        lo = (1 << (step + 1)) - 1
                        dst, dstb = SS(step % 2), [R_A["S"][step % 2]]
                        tt(dst[:, lo:CBW], src[:, lo:CBW], src[:, lo - sh:CBW - sh], ALU.add, srcb, dstb, en=pe_)
                        src, srcb = dst, dstb
                    inv_w = 1.0 / POOL_W[g]
                    srs = src[:, 32 + TP:CBW].rearrange("p (s w) -> p s w", w=64)
                    oth, othb = SS((g + 1) % 2), [R_A["S"][(g + 1) % 2]]
                    if pe_ == "dve":
                        stt(Pc(j, 0, TP), src[:, 32:32 + TP], inv_w, pb[:, 32:32 + TP], ALU.mult, ALU.subtract,
                            srcb + pbb, [Pb(j)])
                        stt(Pc(j, TP, T).rearrange("p (s w) -> p s w", w=32), srs[:, :, 32:64], inv_w,
                            pbs[:, :, 32:64], ALU.mult, ALU.subtract, srcb + pbb, [Pb(j)])
                        tt(oth[:, 0:16], src[:, 32:48], cs("invc", (t * 4 + g) * 16, 16), ALU.mult, srcb + [CSTb], othb)
                        tt(Pc(j, 0, 16), oth[:, 0:16], pb[:, 32:48], ALU.subtract, othb + pbb, [Pb(j)])
                    else:
                        tt(oth[:, 0:16], src[:, 32:48], cs("invc", (t * 4 + g) * 16, 16), ALU.mult, srcb + [CSTb], othb,
                           en="pool")
                        ts(src[:, 32:CBW], src[:, 32:CBW], inv_w, 0.0, ALU.mult, ALU.add, srcb, srcb, en="pool")
                        tt(Pc(j, 0, TP), src[:, 32:32 + TP], pb[:, 32:32 + TP], ALU.subtract, srcb + pbb, [Pb(j)],
                           en="pool")
                        tt(Pc(j, TP, T).rearrange("p (s w) -> p s w", w=32), srs[:, :, 32:64], pbs[:, :, 32:64],
                           ALU.subtract, srcb + pbb, [Pb(j)], en="pool")
                        tt(Pc(j, 0, 16), oth[:, 0:16], pb[:, 32:48], ALU.subtract, othb + pbb, [Pb(j)], en="pool")
                    cwv = lambda k, j=j: cs("cw", j * CONV_K + k, 1)
                    for k in range(2, CONV_K):
                        dst, dstb = ACC(2 * i + k % 2), [ACCb[2 * i + k % 2]]
                        srcw = cb[:, 2 + k:2 + k + AW]
                        stt(dst, srcw, cwv(k), dst, ALU.mult, ALU.add, cbb + [CSTb] + dstb, dstb)
                    sv = lambda a_: a_[:, TP:AW].rearrange("p (s w) -> p s w", w=64)[:, :, 32:64]
                    A0, A1, A0b, A1b = ACC(2 * i), ACC(2 * i + 1), ACCb[2 * i], ACCb[2 * i + 1]
                    tt(Hc(j, 0, TP), A0[:, 0:TP], A1[:, 0:TP], ALU.add, [A0b, A1b], [Hb[j]])
                    tt(Hc(j, TP, T).rearrange("p (s w) -> p s w", w=32), sv(A0), sv(A1), ALU.add, [A0b, A1b], [Hb[j]])
                    sg, sgb = SG(i), [R_A["SG"][i]]
                    ln1, ln2 = LNA(0), LNA(1)
                    if j == 0:
                        cp(ln1, Hc(j), [Hb[j]], [R_A["LN"][0]])
                    else:
                        tt(ln1, ln1, Hc(j), ALU.add, [Hb[j], R_A["LN"][0]], [R_A["LN"][0]])
                    act(sg[:, 0:T], Hc(j), AF.Square, [Hb[j]], sgb)
                    if j == 0:
                        cp(ln2, sg[:, 0:T], sgb, [R_A["LN"][1]])
                    else:
                        tt(ln2, ln2, sg[:, 0:T], ALU.add, sgb + [R_A["LN"][1]], [R_A["LN"][1]])

            sm = next_ps()
            K.group("pe", [lambda e, s=sm: e.matmul(ps0(s), lhsT=ONES_W[:, :], rhs=LNA(0)[:, 0:TP], start=True, stop=True),
                           lambda e, s=sm: e.matmul(ps1(s, 0, SQW), lhsT=ONES_W[:, :], rhs=LNA(0)[:, TP:T], start=True,
                                                    stop=True)], reads=[ONESb, R_A["LN"][0]], writes=[PSb[sm]])
            se = next_ps()
            K.group("pe", [lambda e, s=se: e.matmul(ps0(s), lhsT=ONES_W[:, :], rhs=LNA(1)[:, 0:TP], start=True, stop=True),
                           lambda e, s=se: e.matmul(ps1(s, 0, SQW), lhsT=ONES_W[:, :], rhs=LNA(1)[:, TP:T], start=True,
                                                    stop=True)], reads=[ONESb, R_A["LN"][1]], writes=[PSb[se]])
            MEAN, MEANb = SS(0)[:, 0:T], [R_A["S"][0]]
            RSL, RSLb = SS(1)[:, 0:T], [R_A["S"][1]]
            MSQ, MSQb = SG(0)[:, 0:T], [R_A["SG"][0]]
            cp(MEAN[:, 0:TP], ps0(sm), [PSb[sm]], MEANb)
            cp(MEAN[:, TP:T], ps1(sm, 0, SQW), [PSb[sm]], MEANb)
            tt(MSQ, MEAN, MEAN, ALU.mult, MEANb, MSQb)
            tt(RSL[:, 0:TP], ps0(se), MSQ[:, 0:TP], ALU.subtract, [PSb[se]] + MSQb, RSLb)
            tt(RSL[:, TP:T], ps1(se, 0, SQW), MSQ[:, TP:T], ALU.subtract, [PSb[se]] + MSQb, RSLb)
            rsqrt_eps(RSL, RSL, RSLb, RSLb)
            for j in range(WCC):
                tt(Hc(j), Hc(j), MEAN, ALU.subtract, [Hb[j]] + MEANb, [Hb[j]])
                tt(Hc(j), Hc(j), RSL, ALU.mult, [Hb[j]] + RSLb, [Hb[j]])
                act(ACTC(j), Hc(j), AF.Silu, [Hb[j], CSTb], [ACTCb(j)], bias=cs("lnb", j, 1), scale=cs("lng", j, 1))

            K.switch(allb(R_A), allb(R_M))

            def tb_next():
                i = st["tb"] % nTB
                st["tb"] += 1
                return Hc(i), [Hb[i]]

            r_actc = lambda k, pi: ACTC(k, 0, TP) if pi == 0 else ACTC(k, TP, T)
            for d0 in range(0, DC, 2):
                dd = (d0, d0 + 1)
                sa, sbm = [], []
                wv, wb = load_w(w_in, 0, DC, 3 * WC + d0 * 128, 256)
                for q in range(2):
                    s = mm_job(wv, wb, q * 128, DC, u_main, Ub, mwid)
                    tmp, tmpb = tb_next()
                    act(tmp[:, 0:TP], ps0(s), AF.Sigmoid, [PSb[s]], tmpb)
                    act(tmp[:, TP:T], ps1(s, 0, SQW), AF.Sigmoid, [PSb[s]], tmpb)
                    sa.append((tmp, tmpb))
                wv, wb = load_w(w_in, 0, DC, 3 * WC + D + d0 * 128, 256)
                for q in range(2):
                    s = mm_job(wv, wb, q * 128, DC, u_main, Ub, mwid)
                    tmp, tmpb = tb_next()
                    act(tmp[:, 0:TP], ps0(s), AF.Sigmoid, [PSb[s]], tmpb)
                    act(tmp[:, TP:T], ps1(s, 0, SQW), AF.Sigmoid, [PSb[s]], tmpb)
                    sbm.append((tmp, tmpb))
                g = d0 // GOC
                (wv, wvp), wb = load_w_pack([(w_co, 0, WCC, d0 * 128, 256),
                                             (w_pl, g * GCC, GCC, (d0 % GOC) * 128, 256)])
                for q in range(2):
                    s = mm_job(wv, wb, q * 128, WCC, r_actc, [ACTCb(k) for k in range(0, WCC, 2)], mwid)
                    tmp, tmpb = sa[q]
                    tt(tmp[:, 0:TP], ps0(s), tmp[:, 0:TP], ALU.mult, [PSb[s]] + tmpb, tmpb)
                    tt(tmp[:, TP:T], ps1(s, 0, SQW), tmp[:, TP:T], ALU.mult, [PSb[s]] + tmpb, tmpb)
                wv = wvp
                r_p = lambda k, pi, g=g: Pc(g * GCC + k, 0, TP) if pi == 0 else Pc(g * GCC + k, TP, T)
                for q, d in enumerate(dd):
                    s = mm_job(wv, wb, q * 128, GCC, r_p, [Pb(g * GCC + k) for k in range(GCC)], mwid)
                    tmp, tmpb = sbm[q]
                    psc = cs("psc", d, 1)
                    stt(tmp[:, 0:TP], ps0(s), psc, tmp[:, 0:TP], ALU.mult, ALU.mult, [PSb[s], CSTb] + tmpb, tmpb)
                    stt(tmp[:, TP:T], ps1(s, 0, SQW), psc, tmp[:, TP:T], ALU.mult, ALU.mult, [PSb[s], CSTb] + tmpb,
                        tmpb)
                    tt(Mc(d), sa[q][0], tmp, ALU.add, sa[q][1] + tmpb, [R_M["M"][d]])
                ada_some(ADA_B)

            ada_flush(2)
            HR3 = HR[:, :].rearrange("p (c t) -> p c t", t=T)
            for qd in range(DC // 8):
                c0, c1 = qd * 8, qd * 8 + 8
                dma(HR3[:, c0:c1, 0:TP], xT[t, :, c0:c1, 0:TP], f"hl{qd}", [], Hb[c0:c1])
                dma(HR3[:, c0:c1, TP:T], xT[t, :, c0:c1, TP + 32:TH], f"hl{qd}", [], Hb[c0:c1])

            r_m = lambda k, pi: Mc(k, 0, TP) if pi == 0 else Mc(k, TP, T)
            for d0 in range(0, DC, 2):
                wv, wb = load_w(w_out, 0, DC, d0 * 128, 256)
                for q in range(2):
                    d = d0 + q
                    s = mm_job(wv, wb, q * 128, DC, r_m, R_M["M"], mwid)
                    for (a, b, r) in G_M:
                        src = ps0(s)[:, a:b] if a < TP else ps1(s, a - TP, b - TP)
                        stt(Hc(d, a, b), src, modv(2, r, d), Hc(d, a, b), ALU.mult, ALU.add, [PSb[s], MODb[2], Hb[d]],
                            [Hb[d]])

            ada_flush(4)
            K.switch(allb(R_M), allb(R_norm))
            rms_stats(lambda c: (Hc(c), [Hb[c]]), T, SQW)
            for c in range(DC):
                tmp, tmpb = SQ(c % 2)[:, :T], [R_norm["SQ"][c % 2]]
                tt(tmp, Hc(c), RSTD[:, :T], ALU.mult, [Hb[c]] + R_norm["RS"], tmpb)
                for (a, b, r) in G_M:
                    act(U[:, c, ucol(a):ucol(a) + (b - a)], tmp[:, a:b], AF.Identity, tmpb + [MODb[3], MODb[4]],
                        [Ub[c]], bias=modv(3, r, c), scale=modv(4, r, c))

            ada_flush(5)
            K.switch(allb(R_norm), allb(R_F))
            nblk = (FC + FB - 1) // FB
            for bi in range(nblk):
                f0, f1 = bi * FB, min(FC, (bi + 1) * FB)
                nb = f1 - f0
                hb = bi % 2
                afb = [R_F["AF"][hb]]
                i = f0
                while i < f1:
                    n2 = 2 if i + 1 < f1 else 1
                    wv, wb = load_w(w_fi, 0, DC, i * 128, 128 * n2)
                    sgs = [mm_job(wv, wb, q * 128, DC, u_main, Ub, mwid) for q in range(n2)]
                    for q in range(n2):
                        s = sgs[q]
                        act(SIL(q)[:, 0:TP], ps0(s), AF.Silu, [PSb[s]], [R_F["SIL"][q]])
                        act(SIL(q)[:, TP:T], ps1(s, 0, SQW), AF.Silu, [PSb[s]], [R_F["SIL"][q]])
                    wv, wb = load_w(w_fi, 0, DC, DFF + i * 128, 128 * n2)
                    sus = [mm_job(wv, wb, q * 128, DC, u_main, Ub, mwid) for q in range(n2)]
                    for q in range(n2):
                        s = sus[q]
                        li = i + q - f0
                        tt(AFc(hb, li, 0, TP), SIL(q)[:, 0:TP], ps0(s), ALU.mult, [PSb[s], R_F["SIL"][q]], afb)
                        tt(AFc(hb, li, TP, T), SIL(q)[:, TP:T], ps1(s, 0, SQW), ALU.mult, [PSb[s], R_F["SIL"][q]], afb)
                    i += n2
                r_af = lambda k, pi, hb=hb: AFc(hb, k, 0, TP) if pi == 0 else AFc(hb, k, TP, T)
                for d0 in range(0, DC, 4):
                    wv, wb = load_w(w_fo, f0, nb, d0 * 128, 512)
                    for q in range(4):
                        d = d0 + q
                        s = mm_job(wv, wb, q * 128, nb, r_af, afb, mwid)
                        for (a, b, r) in G_M:
                            src = ps0(s)[:, a:b] if a < TP else ps1(s, a - TP, b - TP)
                            stt(Hc(d, a, b), src, modv(5, r, d), Hc(d, a, b), ALU.mult, ALU.add,
                                [PSb[s], MODb[5], Hb[d]], [Hb[d]])

            K.switch(allb(R_F), allb(R_norm))
            rms_stats(lambda c: (Hc(c), [Hb[c]]), T, SQW)
            for qd in range(DC // 8):
                c0, c1 = qd * 8, qd * 8 + 8
                for c in range(c0, c1):
                    stt(Hc(c), Hc(c), cs("gf", c, 1), RSTD[:, :T], ALU.mult, ALU.mult, [Hb[c], CSTb] + R_norm["RS"],
                        [Hb[c]])
                dma(yT[t, :, c0:c1, :], HR3[:, c0:c1, :], f"ys{qd}", Hb[c0:c1], [])
                out_sems.add(f"ys{qd}")
            K.switch(allb(R_norm), allb(R_F))

        final_waits = [(k, K.dmacnt[k]) for k in sorted(out_sems)]
        K.eng["sp"].q.append((final_waits, None, None))

        keys = ["pe", "act", "dve", "pool"] + sorted(K.dmacnt.keys())
        sems = {k: es.enter_context(nc.semaphore(k)) for k in keys}

        def replay(name):
            def run(e):
                for waits, fn, inc in K.eng[name].q:
                    for (k, v) in waits:
                        e.wait_ge(sems[k], v)
                    if fn is None:
                        continue
                    ins = fn(e)
                    if inc is not None:
                        ins.then_inc(sems[inc[0]], inc[1])
            return run

        with nc.Block() as block:
            block.sync(replay("sp"))
            block.gpsimd(replay("pool"))
            block.tensor(replay("pe"))
            block.scalar(replay("act"))
            block.vector(replay("dve"))
    info = dict(DC=DC, WCC=WCC, T=T, TH=TH, off=off, NCST=NCST, FB=FB,
                n_instr={k: len(v.q) for k, v in K.eng.items()})
    return nc, info


_CACHE = {}


def _host_inputs(D, DFF, TP, NT, inp):
    DC, WC = D // 128, D // 2
    WCC = WC // 128
    T, TH = TP + 64, TP + 96
    f32 = np.float32
    xp = np.asarray(inp["x_prompt"], f32)[0]
    xs = np.asarray(inp["x_sample"], f32)
    sc = np.asarray(inp["state_conv"], f32)[0]
    spl = np.asarray(inp["state_pool"], f32)[0]
    fm = lambda v, n: np.ascontiguousarray(np.asarray(v, f32).reshape(n, 128).T)
    shared = dict(
        w_ada=np.ascontiguousarray(np.asarray(inp["w_ada"], f32)[0]),
        w_in=np.ascontiguousarray(np.asarray(inp["w_in"], f32)[0]),
        w_co=np.ascontiguousarray(np.asarray(inp["w_conv_out"], f32)[0]),
        w_pl=np.ascontiguousarray(np.asarray(inp["w_pool"], f32)[0].reshape(-1, D // 4)),
        w_out=np.ascontiguousarray(np.asarray(inp["w_out"], f32)[0]),
        w_fi=np.ascontiguousarray(np.asarray(inp["w_ffn_in"], f32)[0]),
        w_fo=np.ascontiguousarray(np.asarray(inp["w_ffn_out"], f32)[0]),
    )
    bada = fm(inp["b_ada"][0], 6 * DC)
    g1, g2, gf = fm(inp["g_norm1"][0], DC), fm(inp["g_norm2"][0], DC), fm(inp["g_final"], DC)
    psc = fm(inp["pool_scale"][0], DC)
    cw = np.asarray(inp["conv_w"], f32)[0]
    cwT = np.ascontiguousarray(cw.T.reshape(WCC, 128, CONV_K).transpose(1, 0, 2)).reshape(128, WCC * CONV_K)
    cb, lng, lnb = fm(inp["conv_b"][0], WCC), fm(inp["ln_g"][0], WCC), fm(inp["ln_b"][0], WCC)
    maps = []
    for c in range(NCORES):
        xt = np.zeros((NT, TH, D), f32)
        hm = np.ones((128, NT), f32)
        invc = np.zeros((128, NT, 4, 16), f32)
        stc = np.zeros((NT, 128, WCC, 2, 32), f32)
        stp = np.zeros((NT, 128, WCC, 2, 32), f32)
        for t in range(NT):
            start = (c * NT + t) * TP
            xt[t, 0:TP] = xp[start:start + TP]
            if start >= 32:
                xt[t, TP:TP + 32] = xp[start - 32:start]
            else:
                hm[:, t] = 0.0
            for s in range(2):
                q = c * 2 * NT + 2 * t + s
                xt[t, TP + 32 + 32 * s:TP + 64 + 32 * s] = xs[q]
                stc[t, :, :, s, 2:32] = sc[q].T.reshape(WCC, 128, 30).transpose(1, 0, 2)
                stp[t, :, :, s, 17:32] = spl[q].T.reshape(WCC, 128, 15).transpose(1, 0, 2)
            pos = start + np.arange(16)
            for g, w in enumerate(POOL_W):
                invc[:, t, g, :] = (1.0 / np.minimum(float(w), pos + 1.0)).astype(f32)[None, :]
        xTc = np.ascontiguousarray(xt.reshape(NT, TH, DC, 128).transpose(0, 3, 2, 1))
        crow = np.concatenate([np.asarray(inp["c_prompt"], f32), np.asarray(inp["c_sample"], f32)[4 * c:4 * c + 4]]
                              if NT == 2 else [np.asarray(inp["c_prompt"], f32)], axis=0)
        cT = np.ascontiguousarray(crow.reshape(5, DC, 128).transpose(2, 1, 0)).reshape(128, DC * 5)
        cst = np.concatenate([cT, bada, g1, g2, gf, psc, cwT, cb, lng, lnb, hm, invc.reshape(128, -1),
                              np.full((128, 1), EPS, f32)], axis=1)
        m = dict(xT=xTc, cst=np.ascontiguousarray(cst.astype(f32)), stc=stc, stp=stp)
        m.update(shared)
        maps.append(m)
    return maps


def kernel(**inp):
    D = inp["x_prompt"].shape[2]
    SEQ = inp["x_prompt"].shape[1]
    DFF = inp["w_ffn_out"].shape[1]
    NT = 2
    TP = SEQ // (NCORES * NT)
    NSEQ = inp["x_sample"].shape[0]
    assert NSEQ == NCORES * 2 * NT and inp["x_sample"].shape[1] == 32
    key = (D, DFF, TP, NT)
    if key not in _CACHE:
        _CACHE[key] = build(D, DFF, TP, NT)
    nc, info = _CACHE[key]
    maps = _host_inputs(D, DFF, TP, NT, inp)
    assert maps[0]["cst"].shape[1] == info["NCST"]
    res = run_bass_kernel_spmd(nc, maps, core_ids=list(range(NCORES)))
    DC, WC = D // 128, D // 2
    WCC = WC // 128
    f32 = np.float32
    y_p = np.zeros((1, SEQ, D), f32)
    y_s = np.zeros((NSEQ, 32, D), f32)
    ncs = np.zeros((1, NSEQ, 30, WC), f32)
    nps = np.zeros((1, NSEQ, 15, WC), f32)
    for c in range(NCORES):
        r = res.results[c]
        yT = np.asarray(r["yT"])
        y = yT.transpose(0, 3, 2, 1).reshape(NT, -1, D)
        ocs = np.asarray(r["oc_s"])
        ops = np.asarray(r["op_s"])
        for t in range(NT):
            start = (c * NT + t) * TP
            y_p[0, start:start + TP] = y[t, 0:TP]
            for s in range(2):
                q = c * 2 * NT + 2 * t + s
                y_s[q] = y[t, TP + 32 * s:TP + 32 * s + 32]
                ncs[0, q] = ocs[t, :, :, s, :].transpose(2, 1, 0).reshape(30, WC)
                nps[0, q] = ops[t, :, :, s, :].transpose(2, 1, 0).reshape(15, WC)
    last = res.results[NCORES - 1]
    ncp = np.asarray(last["oc_p"])[NT - 1].transpose(2, 1, 0).reshape(1, 1, 30, WC).astype(f32)
    npp = np.asarray(last["op_p"])[NT - 1].transpose(2, 1, 0).reshape(1, 1, 15, WC).astype(f32)
    return (y_p, y_s, np.ascontiguousarray(ncp), np.ascontiguousarray(npp), ncs, nps)
```
